# Optimizing a Trainium2 kernel written in Bass

```python
import jax, jax.numpy as jnp
from jax import lax
import numpy as np

D_MODEL = 1024
BATCH = 1
SEQ = 16384
DEPTH = 4

CHUNK = 64
MEM_LEN = 256
D_MIX = D_MODEL
HEAD_DIM = 64
SSM_WIDTH = D_MIX // 2
SSM_HEADS = SSM_WIDTH // HEAD_DIM
SSM_GROUPS = 2
SSM_STATE = 128
CONV_WIDTH = 4
CONV_DIM = SSM_WIDTH + 2 * SSM_GROUPS * SSM_STATE
SSM_IN = CONV_DIM + SSM_WIDTH + SSM_HEADS
RWKV_WIDTH = D_MIX // 4
RWKV_HEADS = RWKV_WIDTH // HEAD_DIM
DECAY_LORA = 64
AAA_LORA = 64
RWKV_IN = 4 * RWKV_WIDTH + DECAY_LORA + AAA_LORA
XATTN_WIDTH = D_MIX - SSM_WIDTH - RWKV_WIDTH
XATTN_HEADS = XATTN_WIDTH // HEAD_DIM
XATTN_IN = 2 * XATTN_WIDTH
IN_WIDTH = SSM_IN + RWKV_IN + XATTN_IN
NORM_EPS = 1e-6
LNX_EPS = 64e-5
L2_EPS = 1e-12

kernel_name = "hymba_ssd_rwkv7_memxattn_trunk"


def rmsnorm(x, w):
    xf = x.astype(jnp.float32)
    y = xf * lax.rsqrt(jnp.mean(xf * xf, axis=-1, keepdims=True) + NORM_EPS)
    return (y * w.astype(jnp.float32)).astype(x.dtype)


def causal_dwconv(u, w, b):
    L = u.shape[1]
    up = jnp.pad(u, ((0, 0), (CONV_WIDTH - 1, 0), (0, 0)))
    out = b
    for j in range(CONV_WIDTH):
        out = out + up[:, j:j + L, :] * w[j]
    return out


def ssd_chunked(xs, dt, A, Bg, Cg):
    b, l, h, p = xs.shape
    g, n = Bg.shape[2], Bg.shape[3]
    nc = l // CHUNK
    rep = h // g
    Bh = jnp.repeat(Bg, rep, axis=2).reshape(b, nc, CHUNK, h, n)
    Ch = jnp.repeat(Cg, rep, axis=2).reshape(b, nc, CHUNK, h, n)
    xdt = (xs * dt[..., None]).reshape(b, nc, CHUNK, h, p)
    a_cs = jnp.cumsum((dt * A).reshape(b, nc, CHUNK, h), axis=2)
    seg = a_cs[:, :, :, None, :] - a_cs[:, :, None, :, :]
    causal = jnp.tril(jnp.ones((CHUNK, CHUNK), dtype=bool))[None, None, :, :, None]
    decay_qs = jnp.exp(jnp.where(causal, seg, -jnp.inf))
    scores = jnp.einsum('bcqhn,bcshn->bcqsh', Ch, Bh) * decay_qs
    y_diag = jnp.einsum('bcqsh,bcshp->bcqhp', scores, xdt)
    decay_to_end = jnp.exp(a_cs[:, :, -1:, :] - a_cs)
    states = jnp.einsum('bcqhn,bcqhp->bchpn', Bh * decay_to_end[..., None], xdt)
    chunk_decay = jnp.exp(a_cs[:, :, -1, :])

    def step(carry, inp):
        st, dec = inp
        return carry * dec[:, :, None, None] + st, carry

    init = jnp.zeros((b, h, p, n), jnp.float32)
    _, prev = lax.scan(step, init, (jnp.moveaxis(states, 1, 0), jnp.moveaxis(chunk_decay, 1, 0)))
    prev = jnp.moveaxis(prev, 0, 1)
    y_off = jnp.einsum('bcqhn,bchpn->bcqhp', Ch * jnp.exp(a_cs)[..., None], prev)
    return (y_diag + y_off).reshape(b, l, h, p)


def mamba2_group(u, conv_w, conv_b, dt_bias, a_log, d_skip, norm_w):
    b, l, _ = u.shape
    xbc = jax.nn.silu(causal_dwconv(u[..., :CONV_DIM], conv_w, conv_b)).astype(jnp.float32)
    z = u[..., CONV_DIM:CONV_DIM + SSM_WIDTH].astype(jnp.float32)
    dt_raw = u[..., CONV_DIM + SSM_WIDTH:].astype(jnp.float32)
    xs = xbc[..., :SSM_WIDTH].reshape(b, l, SSM_HEADS, HEAD_DIM)
    Bg = xbc[..., SSM_WIDTH:SSM_WIDTH + SSM_GROUPS * SSM_STATE].reshape(b, l, SSM_GROUPS, SSM_STATE)
    Cg = xbc[..., SSM_WIDTH + SSM_GROUPS * SSM_STATE:].reshape(b, l, SSM_GROUPS, SSM_STATE)
    dt = jax.nn.softplus(dt_raw + dt_bias.astype(jnp.float32))
    A = -jnp.exp(a_log.astype(jnp.float32))
    y = ssd_chunked(xs, dt, A, Bg, Cg) + d_skip.astype(jnp.float32)[:, None] * xs
    y = y.reshape(b, l, SSM_WIDTH) * jax.nn.silu(z)
    yg = y.reshape(b, l, SSM_GROUPS, SSM_WIDTH // SSM_GROUPS)
    yg = yg * lax.rsqrt(jnp.mean(yg * yg, axis=-1, keepdims=True) + NORM_EPS)
    y = yg.reshape(b, l, SSM_WIDTH) * norm_w.astype(jnp.float32)
    return y.astype(u.dtype)


def rwkv7_recurrence(r, w, k, v, a_vec, b_vec):
    b, l, h, d = r.shape

    def step(S, inp):
        r_t, w_t, k_t, v_t, a_t, b_t = inp
        sa = jnp.einsum('bhij,bhj->bhi', S, a_t)
        S = S * w_t[:, :, None, :] + sa[..., None] * b_t[:, :, None, :] + v_t[..., None] * k_t[:, :, None, :]
        return S, jnp.einsum('bhij,bhj->bhi', S, r_t)

    S0 = jnp.zeros((b, h, d, d), jnp.float32)
    xs = (jnp.moveaxis(r, 1, 0), jnp.moveaxis(w, 1, 0), jnp.moveaxis(k, 1, 0),
          jnp.moveaxis(v, 1, 0), jnp.moveaxis(a_vec, 1, 0), jnp.moveaxis(b_vec, 1, 0))
    _, ys = lax.scan(step, S0, xs)
    return jnp.moveaxis(ys, 0, 1)


def rwkv7_group(u, mu, w0, w2, a0, a2, k_k, k_a, r_k, lnx_w, lnx_b):
    b, l, _ = u.shape
    prev = jnp.pad(u, ((0, 0), (1, 0), (0, 0)))[:, :l, :]
    us = (u + (prev - u) * mu).astype(jnp.float32)
    W = RWKV_WIDTH
    r, k, v, g = us[..., :W], us[..., W:2 * W], us[..., 2 * W:3 * W], us[..., 3 * W:4 * W]
    w_lat = us[..., 4 * W:4 * W + DECAY_LORA]
    a_lat = us[..., 4 * W + DECAY_LORA:]
    w_log = -jax.nn.softplus(-(w0.astype(jnp.float32) + jnp.tanh(w_lat) @ w2.astype(jnp.float32))) - 0.5
    decay = jnp.exp(-jnp.exp(w_log))
    a = jax.nn.sigmoid(a0.astype(jnp.float32) + a_lat @ a2.astype(jnp.float32))
    kk = (k * k_k.astype(jnp.float32)).reshape(b, l, RWKV_HEADS, HEAD_DIM)
    kk = kk / jnp.maximum(jnp.sqrt(jnp.sum(kk * kk, axis=-1, keepdims=True)), L2_EPS)
    k = k * (1.0 + (a - 1.0) * k_a.astype(jnp.float32))
    hs = lambda t: t.reshape(b, l, RWKV_HEADS, HEAD_DIM)
    rh, kh, vh, ah = hs(r), hs(k), hs(v), hs(a)
    y = rwkv7_recurrence(rh, hs(decay), kh, vh, -kk, kk * ah)
    mean = jnp.mean(y, axis=-1, keepdims=True)
    var = jnp.mean(jnp.square(y - mean), axis=-1, keepdims=True)
    y = (y - mean) * lax.rsqrt(var + LNX_EPS)
    y = y * lnx_w.astype(jnp.float32).reshape(RWKV_HEADS, HEAD_DIM) \
        + lnx_b.astype(jnp.float32).reshape(RWKV_HEADS, HEAD_DIM)
    bonus = jnp.sum(rh * kh * r_k.astype(jnp.float32), axis=-1, keepdims=True) * vh
    y = (y + bonus).reshape(b, l, W) * jax.nn.silu(g)
    return y.astype(u.dtype)


def memory_xattn_group(u, mem_k, mem_v):
    b, l, _ = u.shape
    q = u[..., :XATTN_WIDTH].reshape(b, l, XATTN_HEADS, HEAD_DIM)
    g = u[..., XATTN_WIDTH:]
    s = jnp.einsum('blhd,bmhd->bhlm', q.astype(jnp.float32), mem_k.astype(jnp.float32)) * (HEAD_DIM ** -0.5)
    p = jax.nn.softmax(s, axis=-1)
    o = jnp.einsum('bhlm,bmhd->blhd', p, mem_v.astype(jnp.float32)).reshape(b, l, XATTN_WIDTH)
    return (o * jax.nn.silu(g.astype(jnp.float32))).astype(u.dtype)


def setup_inputs(seed: int = 0) -> dict:
    key = jax.random.key(seed)
    ks = jax.random.split(key, 24)
    f32 = jnp.float32
    nrm = lambda k, shape, scale: jax.random.normal(k, shape, f32) * scale
    dt = jnp.exp(jax.random.uniform(ks[8], (DEPTH, SSM_HEADS), f32)
                 * (jnp.log(0.1) - jnp.log(0.001)) + jnp.log(0.001))
    return {
        "x": nrm(ks[0], (BATCH, SEQ, D_MODEL), 1.0),
        "mem": nrm(ks[1], (BATCH, MEM_LEN, D_MODEL), 1.0),
        "mem_norm_w": 1.0 + nrm(ks[2], (D_MODEL,), 0.02),
        "w_mem_kv": nrm(ks[3], (D_MODEL, 2 * XATTN_WIDTH), D_MODEL ** -0.5),
        "pre_norm_w": 1.0 + nrm(ks[4], (DEPTH, D_MODEL), 0.02),
        "w_in": nrm(ks[5], (DEPTH, D_MODEL, IN_WIDTH), D_MODEL ** -0.5),
        "conv_w": nrm(ks[6], (DEPTH, CONV_WIDTH, CONV_DIM), 0.5),
        "conv_b": nrm(ks[7], (DEPTH, CONV_DIM), 0.02),
        "dt_bias": dt + jnp.log(-jnp.expm1(-dt)),
        "a_log": jnp.log(jax.random.uniform(ks[9], (DEPTH, SSM_HEADS), f32, 1.0, 16.0)),
        "d_skip": 1.0 + nrm(ks[10], (DEPTH, SSM_HEADS), 0.1),
        "ssm_norm_w": 1.0 + nrm(ks[11], (DEPTH, SSM_WIDTH), 0.02),
        "shift_mu": jax.random.uniform(ks[12], (DEPTH, RWKV_IN), f32),
        "w0": jax.random.uniform(ks[13], (DEPTH, RWKV_WIDTH), f32, -4.0, 1.0),
        "w2": nrm(ks[14], (DEPTH, DECAY_LORA, RWKV_WIDTH), 0.1),
        "a0": nrm(ks[15], (DEPTH, RWKV_WIDTH), 0.1),
        "a2": nrm(ks[16], (DEPTH, AAA_LORA, RWKV_WIDTH), 0.3 * AAA_LORA ** -0.5),
        "k_k": 0.85 + nrm(ks[17], (DEPTH, RWKV_WIDTH), 0.02),
        "k_a": 1.0 + nrm(ks[18], (DEPTH, RWKV_WIDTH), 0.02),
        "r_k": nrm(ks[19], (DEPTH, RWKV_HEADS, HEAD_DIM), 0.1),
        "lnx_w": 1.0 + nrm(ks[20], (DEPTH, RWKV_WIDTH), 0.02),
        "lnx_b": nrm(ks[21], (DEPTH, RWKV_WIDTH), 0.02),
        "w_out": nrm(ks[22], (DEPTH, D_MIX, D_MODEL), D_MIX ** -0.5),
        "post_norm_w": 1.0 + nrm(ks[23], (DEPTH, D_MODEL), 0.02),
    }


def reference(x, mem, mem_norm_w, w_mem_kv, pre_norm_w, w_in, conv_w, conv_b, dt_bias, a_log,
              d_skip, ssm_norm_w, shift_mu, w0, w2, a0, a2, k_k, k_a, r_k, lnx_w, lnx_b,
              w_out, post_norm_w):
    b = mem.shape[0]
    kv = rmsnorm(mem, mem_norm_w) @ w_mem_kv
    mem_k = kv[..., :XATTN_WIDTH].reshape(b, MEM_LEN, XATTN_HEADS, HEAD_DIM)
    mem_v = kv[..., XATTN_WIDTH:].reshape(b, MEM_LEN, XATTN_HEADS, HEAD_DIM)
    for i in range(DEPTH):
        h = rmsnorm(x, pre_norm_w[i])
        u = h @ w_in[i]
        y_ssm = mamba2_group(u[..., :SSM_IN], conv_w[i], conv_b[i], dt_bias[i], a_log[i],
                             d_skip[i], ssm_norm_w[i])
        y_rwkv = rwkv7_group(u[..., SSM_IN:SSM_IN + RWKV_IN], shift_mu[i], w0[i], w2[i], a0[i],
                             a2[i], k_k[i], k_a[i], r_k[i], lnx_w[i], lnx_b[i])
        y_mem = memory_xattn_group(u[..., SSM_IN + RWKV_IN:], mem_k, mem_v)
        y = jnp.concatenate([y_ssm, y_rwkv, y_mem], axis=-1)
        x = x + rmsnorm(y @ w_out[i], post_norm_w[i])
    return x
```

```python
import numpy as np
import ml_dtypes
from contextlib import ExitStack
from collections import defaultdict
import concourse.bass as bass
import concourse.mybir as mybir
from concourse.bass_utils import run_bass_kernel_spmd

F32 = mybir.dt.float32
BF16 = mybir.dt.bfloat16
AF = mybir.ActivationFunctionType
ALU = mybir.AluOpType
AX = mybir.AxisListType

D = 1024
NCORES = 8
IN_W = 3208
C_Z = 1024
C_DT = 1536
C_RW = 1544
C_Q = 2696
C_G = 2952
LW = -0.6065306597126334
NORM_EPS = 1e-6
LNX_EPS = 64e-5
GEN = 30000

CI, CTRIU, CLS, CONES, CUS, CI2 = 0, 1, 2, 3, 4, 5
NCF = 6
BI, BSH, BE, BE0 = 0, 1, 2, 3
NCB = 4


def host_consts():
    r = np.arange(128)[:, None]
    c = np.arange(128)[None, :]
    cf = np.zeros((128, NCF, 128), np.float32)
    cf[:, CI] = (r == c)
    cf[:, CTRIU] = (r <= c)
    cf[:, CLS] = (r > c)
    cf[:, CONES] = 1.0
    cf[:, CUS] = (r < c)
    cf[:, CI2] = ((r % 64) == (c - 64))
    cb = np.zeros((128, NCB, 128), np.float32)
    cb[:, BI] = (r == c)
    cb[:, BSH] = (c == r + 1)
    cb[:, BE] = (r == 127) & (c == 0)
    cb[:, BE0] = (r == 2) & (c == 0)
    return cf, cb


class Buf:
    def __init__(self, t, key):
        self.t = t
        self.k = key

    def __getitem__(self, idx):
        return self.t[idx]


class Prog:
    def __init__(self, nc, es):
        self.nc = nc
        self.es = es
        self.eng = {'pe': nc.tensor, 'act': nc.scalar, 'dve': nc.vector, 'pool': nc.gpsimd, 'sp': nc.sync}
        self.sems = {}
        self.cnt = defaultdict(int)
        self.waited = defaultdict(int)
        self.state = {}
        self.ndma = 48
        self.dma_sems = [es.enter_context(nc.semaphore("dq%d" % i)) for i in range(self.ndma)]
        self.dma_cnt = [0] * self.ndma
        self.dma_next = 0
        self.ninst = 0
        self.all_dma_tokens = []
        self.bank_rg = {}

    def _sem(self, e, gen):
        k = (e, gen)
        if k not in self.sems:
            self.sems[k] = self.es.enter_context(self.nc.semaphore("c_%s_%d" % (e, gen)))
        return self.sems[k]

    def _st(self, key):
        s = self.state.get(key)
        if s is None:
            s = self.state[key] = [None, []]
        return s

    def _wait(self, E, tok):
        kind, sid, val = tok
        wk = (E, kind, sid)
        if self.waited[wk] >= val:
            return
        self.waited[wk] = val
        if kind == 'c':
            e, gen = sid
            self.eng[E].wait_ge(self._sem(e, gen), val)
            for g in range(gen):
                self.waited[(E, 'c', (e, g))] = 1 << 30
        else:
            self.eng[E].wait_ge(self.dma_sems[sid], val)
        self.ninst += 1

    def _deps(self, E, r, w):
        need = []
        for key in r:
            s = self._st(key)
            if s[0] is not None:
                need.append((s[0], True))
        for key in w:
            s = self._st(key)
            if s[0] is not None:
                need.append((s[0], False))
            for rd in s[1]:
                need.append((rd, False))
        for tok, raw in need:
            if tok[0] == 'c' and tok[1][0] == E and not raw:
                continue
            self._wait(E, tok)

    def _commit(self, tok, r, w):
        for key in r:
            s = self._st(key)
            s[1].append(tok)
            if len(s[1]) > 64:
                s[1] = s[1][-64:]
        for key in w:
            s = self._st(key)
            s[0] = tok
            s[1] = []

    def op(self, E, fn, r=(), w=()):
        r = [b.k if isinstance(b, Buf) else b for b in r]
        w = [b.k if isinstance(b, Buf) else b for b in w]
        self._deps(E, r, w)
        inst = fn(self.eng[E])
        self.cnt[E] += 1
        n = self.cnt[E]
        gen, val = (n - 1) // GEN, (n - 1) % GEN + 1
        inst.then_inc(self._sem(E, gen), 1)
        self.ninst += 1
        self._commit(('c', (E, gen), val), r, w)

    def dma(self, Q, out, in_, r=(), w=(), slow=False):
        r = [b.k if isinstance(b, Buf) else b for b in r]
        w = [b.k if isinstance(b, Buf) else b for b in w]
        self._deps(Q, r, w)
        sid = self.dma_next
        self.dma_next = (self.dma_next + 1) % self.ndma
        if self.dma_cnt[sid] > 0:
            self._wait(Q, ('d', sid, self.dma_cnt[sid]))
        self.dma_cnt[sid] += 16
        if slow:
            self.eng[Q].dma_start(out=out, in_=in_, allow_slow_non_contiguous=True).then_inc(self.dma_sems[sid], 16)
        else:
            self.eng[Q].dma_start(out=out, in_=in_).then_inc(self.dma_sems[sid], 16)
        self.ninst += 1
        tok = ('d', sid, self.dma_cnt[sid])
        self._commit(tok, r, w)
        return tok

    def barrier(self, engines=('pe', 'act', 'dve', 'pool', 'sp')):
        toks = []
        for e in ('pe', 'act', 'dve', 'pool'):
            n = self.cnt[e]
            if n:
                toks.append(('c', (e, (n - 1) // GEN), (n - 1) % GEN + 1))
        for sid in range(self.ndma):
            if self.dma_cnt[sid]:
                toks.append(('d', sid, self.dma_cnt[sid]))
        for E in engines:
            for t in toks:
                self._wait(E, t)

    def mm(self, out, lhsT, rhs, start=True, stop=True, r=(), w=(), rg=None):
        bank = [b.k if isinstance(b, Buf) else b for b in w][0]
        last = self.bank_rg.get(bank)
        if rg is not None and last is not None and last[0] is not None and last[0] != rg:
            self._wait('pe', last[1])
        self.op('pe', lambda e: e.matmul(out, lhsT, rhs, start=start, stop=stop), r, w)
        n = self.cnt['pe']
        self.bank_rg[bank] = (rg, ('c', ('pe', (n - 1) // GEN), (n - 1) % GEN + 1))

    def tr(self, out, in_, ident, r=(), w=()):
        self.op('pe', lambda e: e.transpose(out, in_, ident), r, w)

    def act(self, out, in_, func, r=(), w=(), bias=None, scale=None, accum=None):
        kw = {}
        if bias is not None:
            kw['bias'] = bias
        if scale is not None:
            kw['scale'] = scale
        if accum is not None:
            kw['accum_out'] = accum
        self.op('act', lambda e: e.activation(out, in_, func, **kw), r, w)

    def tt(self, E, out, a, b, op, r=(), w=()):
        self.op(E, lambda e: e.tensor_tensor(out, a, b, op), r, w)

    def ts(self, E, out, a, s1, op0, s2=None, op1=None, r=(), w=()):
        if op1 is None:
            self.op(E, lambda e: e.tensor_scalar(out, a, s1, None, op0), r, w)
        else:
            self.op(E, lambda e: e.tensor_scalar(out, a, s1, s2, op0, op1), r, w)

    def stt(self, E, out, a, s, b, op0, op1, r=(), w=()):
        self.op(E, lambda e: e.scalar_tensor_tensor(out, a, s, b, op0, op1), r, w)

    def cp(self, E, out, in_, r=(), w=()):
        if E == 'act':
            self.op('act', lambda e: e.copy(out, in_), r, w)
        else:
            self.op(E, lambda e: e.tensor_copy(out, in_), r, w)

    def memset(self, E, ap, val, w=()):
        self.op(E, lambda e: e.memset(ap, val), (), w)


def bc(ap, shape, axis):
    return ap.unsqueeze(axis).to_broadcast(list(shape))


class StopBuild(Exception):
    pass


class Ctx:
    stop = None

    def ck(self, name):
        if self.stop == name:
            raise StopBuild(name)


def make_ctx(nc, es, NT):
    c = Ctx()
    c.nc, c.es, c.NT, c.T = nc, es, NT, NT * 128
    c.P = Prog(nc, es)
    c.banks = [Buf(es.enter_context(nc.psum_tensor("pb%d" % i, [128, 512], F32)), "pb%d" % i) for i in range(8)]
    c.bank_i = 0
    return c


def sb(c, name, shape, dt, es=None):
    return Buf((es or c.es).enter_context(c.nc.sbuf_tensor(name, shape, dt)), name)


def pb(c):
    b = c.banks[c.bank_i]
    c.bank_i = (c.bank_i + 1) % 8
    return b


def v3(ap, **kw):
    return ap.rearrange("p (a b) -> p a b", **kw)


def load_consts(c, cf_d, cb_d):
    P = c.P
    c.cf = sb(c, "cf_sb", [128, NCF, 128], F32)
    c.cb = sb(c, "cb_sb", [128, NCB, 128], BF16)
    P.dma('sp', c.cf[:], cf_d, w=[c.cf])
    P.dma('pool', c.cb[:], cb_d, w=[c.cb])
    c.cfb = sb(c, "cfb_sb", [128, NCF, 128], BF16)
    P.cp('dve', c.cfb[:], c.cf[:], r=[c.cf], w=[c.cfb])


def rms_rstd(c, ss, rstd, n, parts=128):
    P = c.P
    P.ts('dve', ss[0:parts, 0:1], ss[0:parts, 0:1], 1.0 / n, ALU.mult, NORM_EPS, ALU.add, r=[ss], w=[ss])
    P.act(ss[0:parts, 0:1], ss[0:parts, 0:1], AF.Sqrt, r=[ss], w=[ss])
    P.op('dve', lambda e: e.reciprocal(rstd[0:parts, 0:1], ss[0:parts, 0:1]), r=[ss], w=[rstd])


def mem_kv(c, mem_d, mnw_d, wkv_d, es):
    P = c.P
    c.KT = sb(c, "KT", [128, 2, 256], BF16)
    c.Vm = sb(c, "Vm", [128, 2, 256], BF16)
    wkv = sb(c, "wkv_sb", [128, 8, 512], BF16, es)
    mt = sb(c, "mem_t", [128, 1024], F32, es)
    mj = sb(c, "mem_j", [128, 1024], BF16, es)
    mb = sb(c, "mem_b", [128, 1024], BF16, es)
    mnw = sb(c, "mnw", [128, 8], F32, es)
    hmT = sb(c, "hmT", [128, 8, 256], BF16, es)
    ss = sb(c, "mem_ss", [128, 1], F32, es)
    rs = sb(c, "mem_rs", [128, 1], F32, es)
    for kc in range(8):
        P.dma('pool', wkv[:, kc, :], wkv_d[kc * 128:(kc + 1) * 128, :], w=[wkv])
    P.dma('sp', mnw[:], mnw_d.rearrange("(kc p) -> p kc", p=128), w=[mnw], slow=True)
    for i in range(2):
        P.dma('sp', mt[:], mem_d[i * 128:(i + 1) * 128, :], w=[mt])
        P.act(mj[:], mt[:], AF.Square, r=[mt], w=[mj, ss], accum=ss[:, 0:1])
        rms_rstd(c, ss, rs, 1024)
        P.ts('dve', mb[:], mt[:], rs[:, 0:1], ALU.mult, r=[mt, rs], w=[mb])
        bk = pb(c)
        pbf = bk[:].bitcast(BF16)
        for kc in range(8):
            P.tr(pbf[:, kc * 128:(kc + 1) * 128], mb[:, kc * 128:(kc + 1) * 128], c.cb[:, BI, :], r=[mb, c.cb], w=[bk])
        P.tt('dve', hmT[:, :, i * 128:(i + 1) * 128], v3(pbf[:, 0:1024], a=8), bc(mnw[:], [128, 8, 128], 2), ALU.mult,
             r=[bk, mnw], w=[hmT])
    bk = pb(c)
    for ft in range(2):
        for kc in range(8):
            P.mm(bk[:, ft * 256:(ft + 1) * 256], wkv[:, kc, ft * 128:(ft + 1) * 128], hmT[:, kc, :],
                 start=(kc == 0), stop=(kc == 7), r=[wkv, hmT], w=[bk])
    P.cp('act', c.KT[:], v3(bk[:, 0:512], a=2), r=[bk], w=[c.KT])
    bk = pb(c)
    for mt_ in range(2):
        for kc in range(8):
            P.mm(bk[:, mt_ * 256:(mt_ + 1) * 256], hmT[:, kc, mt_ * 128:(mt_ + 1) * 128], wkv[:, kc, 256:512],
                 start=(kc == 0), stop=(kc == 7), r=[wkv, hmT], w=[bk])
    P.cp('act', c.Vm[:], v3(bk[:, 0:512], a=2), r=[bk], w=[c.Vm])


def phase_A(c, L, x_tile, xh_ap, sc, es):
    P, NT = c.P, c.NT
    cf, cb, cfb = c.cf, c.cb, c.cfb
    S = lambda n, s, d: sb(c, "A_" + n, s, d, es)
    w_in = c.w_in_sb
    for kc in range(8):
        P.dma('pool', w_in[:, kc, :], L['w_in'][kc * 128:(kc + 1) * 128, :], w=[w_in])
    prew = S("prew", [128, 8], F32)
    P.dma('sp', prew[:], L['pre_norm_w'].rearrange("(kc p) -> p kc", p=128), w=[prew], slow=True)
    cw = S("cw", [128, 8, 5], F32)
    for j in range(4):
        P.dma('sp', cw[:, :, j], L['conv_w'][j].rearrange("(ft p) -> p ft", p=128), w=[cw], slow=True)
    P.dma('sp', cw[:, :, 4], L['conv_b'].rearrange("(ft p) -> p ft", p=128), w=[cw], slow=True)
    pA = S("pA", [128, 2456], F32)
    o_mu, o_w0a0, o_kk, o_ka, o_rk, o_dtb, o_alog, o_dsk = 0, 1152, 1664, 1920, 2176, 2432, 2440, 2448
    P.dma('sp', pA[:, o_mu:o_mu + 1152], L['shift_mu'].partition_broadcast(128), w=[pA])
    P.dma('sp', pA[:, o_w0a0:o_w0a0 + 256], L['w0'].partition_broadcast(128), w=[pA])
    P.dma('sp', pA[:, o_w0a0 + 256:o_w0a0 + 512], L['a0'].partition_broadcast(128), w=[pA])
    P.dma('sp', pA[:, o_kk:o_kk + 256], L['k_k'].partition_broadcast(128), w=[pA])
    P.dma('sp', pA[:, o_ka:o_ka + 256], L['k_a'].partition_broadcast(128), w=[pA])
    P.dma('sp', pA[:, o_rk:o_rk + 256], L['r_k'].rearrange("h d -> (h d)").partition_broadcast(128), w=[pA])
    P.dma('sp', pA[:, o_dtb:o_dtb + 8], L['dt_bias'].partition_broadcast(128), w=[pA])
    P.dma('sp', pA[:, o_alog:o_alog + 8], L['a_log'].partition_broadcast(128), w=[pA])
    P.dma('sp', pA[:, o_dsk:o_dsk + 8], L['d_skip'].partition_broadcast(128), w=[pA])
    Abc = S("Abc", [128, 8], F32)
    P.act(Abc[:], pA[:, o_alog:o_alog + 8], AF.Exp, r=[pA], w=[Abc])
    P.ts('dve', Abc[:], Abc[:], -1.0, ALU.mult, r=[Abc], w=[Abc])
    w2a2 = S("w2a2", [128, 256], BF16)
    P.dma('pool', w2a2[0:64, :], L['w2'], w=[w2a2])
    P.dma('pool', w2a2[64:128, :], L['a2'], w=[w2a2])

    c.ck('params')
    junk = S("junk", [128, 1024], BF16)
    hb = S("hb", [128, 1024], BF16)
    ss = S("ss", [128, 1], F32)
    rs = S("rs", [128, 1], F32)
    hT = [S("hT%d" % i, [128, 8, 131], BF16) for i in range(2)]
    uT = S("uT", [128, 8, 131], F32)
    cacc = S("cacc", [128, 8, 128], F32)
    xbcT = S("xbcT", [128, 6, 128], BF16)
    ct = S("ct", [128, 2, 128], BF16)
    szb = S("szb", [128, 512], BF16)
    dts = S("dts", [128, 8, 8], F32)
    xsB = S("xsB", [128, 768], BF16)
    rhsb = S("rhsb", [128, 8, 128], F32)
    Lsb = S("Lsb", [128, 8, 128], F32)
    Gm = S("Gm", [128, 2, 128], F32)
    Mb = S("Mb", [128, 8, 128], BF16)
    xdt = S("xdt", [128, 512], BF16)
    xdt2 = S("xdt2", [128, 512], BF16)
    t1 = S("t1", [128, 512], F32)
    t3 = S("t3", [128, 512], F32)
    ygl = S("ygl", [128, 512], BF16)
    Hs32 = S("Hs32", [128, 512], F32)
    Hsb = S("Hsb", [128, 512], BF16)
    Pc = S("Pc", [128, 8], F32)
    u32 = S("u32", [128, 1152], F32)
    ub = [S("ub%d" % i, [128, 1152], BF16) for i in range(2)]
    ubh = S("ubh", [3, 1152], BF16)
    us = S("us", [128, 1152], F32)
    latb = S("latb", [128, 128], BF16)
    latT = S("latT", [128, 128], BF16)
    pre = S("pre", [128, 512], F32)
    sig = S("sig", [128, 512], F32)
    kk0 = S("kk0", [128, 256], F32)
    kk2 = S("kk2", [128, 256], F32)
    sm4 = S("sm4", [128, 4, 4], F32)
    kk = S("kk", [128, 256], F32)
    k2 = S("k2", [128, 256], F32)
    bb = S("bb", [128, 256], F32)
    cst = S("cst", [128, 512], F32)
    dd = S("dd", [128, 512], F32)
    g4 = S("g4", [128, 4, 256], F32)
    gC = S("gC", [128, 2], F32)
    tok4 = S("tok4", [128, 4, 256], BF16)
    az = S("az", [128, 4, 128], BF16)
    Bz = S("Bz", [128, 4, 128], BF16)
    Kz = S("Kz", [128, 4, 128], BF16)
    Vb = S("Vb", [128, 256], BF16)
    rk = S("rk", [128, 256], F32)
    sgf = S("sgf", [128, 256], F32)
    sgb = S("sgb", [128, 256], BF16)
    bsg = S("bsg", [128, 256], BF16)
    FT = S("FT", [128, 2, 4, 128], BF16)
    Xm1 = S("Xm1", [128, 4, 2, 128], BF16)
    Xm2 = S("Xm2", [128, 4, 2, 128], BF16)
    ArkT = S("ArkT", [128, 4, 128], BF16)
    m2 = S("m2", [128, 2, 128], F32)
    Pp = [S("Pp%d" % i, [128, 4, 128], BF16) for i in range(2)]
    PTp = [S("PTp%d" % i, [128, 4, 128], BF16) for i in range(2)]
    TT32 = S("TT32", [128, 4, 128], F32)
    TTb = [S("TTb%d" % i, [128, 4, 128], BF16) for i in range(2)]
    WT = S("WT", [128, 2, 128], BF16)
    MT = S("MT", [128, 4, 128], BF16)
    Ub = S("Ub", [128, 4, 128], BF16)
    YTb = S("YTb", [128, 4, 128], BF16)
    H32 = S("H32", [128, 2, 128], F32)
    Hb = S("Hb", [128, 2, 128], BF16)
    qTb = S("qTb", [128, 2, 128], BF16)
    sgm = S("sgm", [128, 256], F32)
    xs4 = S("xs4", [128, 4, 4], F32)
    Pb = S("Pb", [128, 4, 256], BF16)
    PTb = S("PTb", [128, 8, 128], BF16)
    ym = S("ym", [128, 256], F32)
    ymb = S("ymb", [128, 256], BF16)

    P.memset('pool', Hs32[:], 0.0, w=[Hs32])
    P.memset('pool', Hsb[:], 0.0, w=[Hsb])
    P.memset('pool', Pc[:], 1.0, w=[Pc])
    P.memset('pool', H32[:], 0.0, w=[H32])
    for ft in range(2):
        P.cp('dve', H32[:, ft, 64:128], cf[:, CI2, 64:128], r=[cf], w=[H32])
    P.cp('dve', Hb[:], H32[:], r=[H32], w=[Hb])
    for z in (az, Bz, Kz):
        P.memset('pool', z[:], 0.0, w=[z])
    P.cp('dve', m2[:, 0, :], cf[:, CUS, :], r=[cf], w=[m2])
    P.cp('dve', m2[:, 1, :], cf[:, CTRIU, :], r=[cf], w=[m2])

    def zview(z):
        return bass.AP(z.t, 0, [[512, 128], [256, 2], [192, 2], [1, 64]])

    c.ck('init')
    xh, xh_keys = xh_ap
    P.act(junk[0:3, :], xh, AF.Square, r=xh_keys, w=[junk, ss], accum=ss[0:3, 0:1])
    rms_rstd(c, ss, rs, 1024, parts=3)
    P.ts('dve', hb[0:3, :], xh, rs[0:3, 0:1], ALU.mult, r=xh_keys + [rs], w=[hb])
    bk = pb(c)
    pbf = bk[:].bitcast(BF16)
    for kc in range(8):
        P.tr(pbf[:, kc * 4:kc * 4 + 3], hb[0:3, kc * 128:(kc + 1) * 128], cb[0:3, BI, 0:3], r=[hb, cb], w=[bk])
    P.tt('dve', hT[0][:, :, 0:3], v3(pbf[:, 0:32], a=8)[:, :, 0:3], bc(prew[:], [128, 8, 3], 2), ALU.mult,
         r=[bk, prew], w=[hT[0]])
    for (c0, c1) in ((0, 512), (512, 1024), (1024, 1152)):
        bk = pb(c)
        for kc in range(8):
            P.mm(bk[0:3, 0:c1 - c0], hT[0][:, kc, 0:3], w_in[:, kc, C_RW + c0:C_RW + c1],
                 start=(kc == 0), stop=(kc == 7), r=[hT[0], w_in], w=[bk])
        P.cp('act', ubh[0:3, c0:c1], bk[0:3, 0:c1 - c0], r=[bk], w=[ubh])

    c.ck('halo')
    for i in range(NT):
        hTc, hTn = hT[i % 2], hT[(i + 1) % 2]
        ubc, ubp = ub[i % 2], ub[(i + 1) % 2]
        tsl = slice(i * 128, (i + 1) * 128)
        xt, xkeys = x_tile(i)
        P.act(junk[:], xt, AF.Square, r=xkeys, w=[junk, ss], accum=ss[:, 0:1])
        rms_rstd(c, ss, rs, 1024)
        P.ts('dve', hb[:], xt, rs[:, 0:1], ALU.mult, r=xkeys + [rs], w=[hb])
        bk = pb(c)
        pbf = bk[:].bitcast(BF16)
        for kc in range(8):
            P.tr(pbf[:, kc * 128:(kc + 1) * 128], hb[:, kc * 128:(kc + 1) * 128], cb[:, BI, :], r=[hb, cb], w=[bk])
        P.tt('dve', hTc[:, :, 3:131], v3(pbf[:, 0:1024], a=8), bc(prew[:], [128, 8, 128], 2), ALU.mult,
             r=[bk, prew], w=[hTc])
        P.cp('pool', hTn[:, :, 0:3], hTc[:, :, 128:131], r=[hTc], w=[hTn])

        c.ck('a1')
        for (f0, f1) in ((0, 3), (3, 6), (6, 8)):
            bk = pb(c)
            for ft in range(f0, f1):
                for kc in range(8):
                    P.mm(bk[:, (ft - f0) * 131:(ft - f0 + 1) * 131], w_in[:, kc, ft * 128:(ft + 1) * 128], hTc[:, kc, :],
                         start=(kc == 0), stop=(kc == 7), r=[w_in, hTc], w=[bk])
            P.cp('act', uT[:, f0:f1, :], v3(bk[:, 0:(f1 - f0) * 131], a=f1 - f0), r=[bk], w=[uT])
        bk = pb(c)
        for kc in range(8):
            P.mm(bk[:, 0:512], hTc[:, kc, 3:131], w_in[:, kc, C_Z:C_Z + 512], start=(kc == 0), stop=(kc == 7),
                 r=[w_in, hTc], w=[bk])
        P.act(szb[:], bk[:, 0:512], AF.Silu, r=[bk], w=[szb])
        P.dma('sp', sc['sz'][tsl, :], szb[:], r=[szb])
        bkm = pb(c)
        for kc in range(8):
            P.mm(bkm[:, 0:8], hTc[:, kc, 3:131], w_in[:, kc, C_DT:C_DT + 8], start=(kc == 0), stop=(kc == 7),
                 r=[w_in, hTc], w=[bkm])
        for kc in range(8):
            P.mm(bkm[:, 256:512], hTc[:, kc, 3:131], w_in[:, kc, C_G:C_G + 256], start=(kc == 0), stop=(kc == 7),
                 r=[w_in, hTc], w=[bkm])
        P.tt('dve', dts[:, 0, :], bkm[:, 0:8], pA[:, o_dtb:o_dtb + 8], ALU.add, r=[bkm, pA], w=[dts])
        P.act(sgm[:], bkm[:, 256:512], AF.Silu, r=[bkm], w=[sgm])
        bkq = pb(c)
        for ft in range(2):
            for kc in range(8):
                P.mm(bkq[:, ft * 128:(ft + 1) * 128], w_in[:, kc, C_Q + ft * 128:C_Q + (ft + 1) * 128], hTc[:, kc, 3:131],
                     start=(kc == 0), stop=(kc == 7), r=[w_in, hTc], w=[bkq])
        P.cp('act', qTb[:], v3(bkq[:, 0:256], a=2), r=[bkq], w=[qTb])
        for (c0, c1) in ((0, 512), (512, 1024), (1024, 1152)):
            bk = pb(c)
            for kc in range(8):
                P.mm(bk[:, 0:c1 - c0], hTc[:, kc, 3:131], w_in[:, kc, C_RW + c0:C_RW + c1],
                     start=(kc == 0), stop=(kc == 7), r=[w_in, hTc], w=[bk])
            P.cp('act', u32[:, c0:c1], bk[:, 0:c1 - c0], r=[bk], w=[u32])
        P.cp('pool', ubc[:], u32[:], r=[u32], w=[ubc])

        c.ck('inproj')
        for ft in range(8):
            E = 'dve'
            P.ts(E, cacc[:, ft, :], uT[:, ft, 0:128], cw[:, ft, 0:1], ALU.mult, cw[:, ft, 4:5], ALU.add,
                 r=[uT, cw], w=[cacc])
            for j in range(1, 4):
                P.stt('dve', cacc[:, ft, :], uT[:, ft, j:j + 128], cw[:, ft, j:j + 1], cacc[:, ft, :], ALU.mult, ALU.add,
                      r=[uT, cw, cacc], w=[cacc])
        P.act(xbcT[:], cacc[:, 0:6, :], AF.Silu, r=[cacc], w=[xbcT])
        P.act(ct[:], cacc[:, 6:8, :], AF.Silu, r=[cacc], w=[ct])
        P.dma('sp', sc['ct'][:, i, :, :], ct[:], r=[ct])

        c.ck('conv')
        P.act(dts[:, 1, :], dts[:, 0, :], AF.Exp, r=[dts], w=[dts])
        P.act(dts[:, 2, :], dts[:, 1, :], AF.Ln, r=[dts], w=[dts], bias=1.0)
        P.tt('dve', dts[:, 3, :], dts[:, 2, :], Abc[:], ALU.mult, r=[dts, Abc], w=[dts])
        bk = pb(c)
        pbf = bk[:].bitcast(BF16)
        for ft in range(6):
            P.tr(pbf[:, ft * 128:(ft + 1) * 128], xbcT[:, ft, :], cb[:, BI, :], r=[xbcT, cb], w=[bk])
        P.cp('act', xsB[:], pbf[:, 0:768], r=[bk], w=[xsB])
        P.tt('pool', rhsb[:], bc(cf[:, CTRIU, :], [128, 8, 128], 1), bc(dts[:, 3, :], [128, 8, 128], 2), ALU.mult,
             r=[cf, dts], w=[rhsb])
        bkm = pb(c)
        P.mm(bkm[:, 0:8], cf[:, CTRIU, :], dts[:, 3, :], r=[cf, dts], w=[bkm])
        P.mm(bkm[:, 8:16], cf[:, CONES, :], dts[:, 3, :], r=[cf, dts], w=[bkm])
        P.act(dts[:, 4, :], bkm[:, 0:8], AF.Exp, r=[bkm], w=[dts])
        P.act(dts[:, 5, :], bkm[:, 8:16], AF.Exp, r=[bkm], w=[dts])
        for hf in range(2):
            bk = pb(c)
            P.mm(bk[:, 0:512], cf[:, CLS, :], rhsb[:, hf * 4:(hf + 1) * 4, :], r=[cf, rhsb], w=[bk])
            P.act(Lsb[:, hf * 4:(hf + 1) * 4, :], v3(bk[:, 0:512], a=4), AF.Exp, r=[bk], w=[Lsb])
        bk = pb(c)
        for g in range(2):
            P.mm(bk[:, g * 128:(g + 1) * 128], xbcT[:, 4 + g, :], ct[:, g, :], r=[xbcT, ct], w=[bk])
        P.tt('dve', Gm[:], v3(bk[:, 0:256], a=2), bc(cf[:, CTRIU, :], [128, 2, 128], 1), ALU.mult, r=[bk, cf], w=[Gm])
        for g in range(2):
            P.tt('pool', Mb[:, g * 4:(g + 1) * 4, :], Lsb[:, g * 4:(g + 1) * 4, :], bc(Gm[:, g, :], [128, 4, 128], 1),
                 ALU.mult, r=[Lsb, Gm], w=[Mb])
        P.tt('dve', v3(xdt[:], a=8), v3(xsB[:, 0:512], a=8), bc(dts[:, 2, :], [128, 8, 64], 2), ALU.mult,
             r=[xsB, dts], w=[xdt])
        P.tt('dve', v3(xdt2[:], a=8), v3(xdt[:], a=8), bc(Lsb[:, :, 127], [128, 8, 64], 2), ALU.mult,
             r=[xdt, Lsb], w=[xdt2])
        bky = pb(c)
        for h in range(8):
            P.mm(bky[:, h * 64:(h + 1) * 64], Mb[:, h, :], xdt[:, h * 64:(h + 1) * 64], r=[Mb, xdt], w=[bky])
        bko = pb(c)
        for g in range(2):
            P.mm(bko[:, g * 256:(g + 1) * 256], ct[:, g, :], Hsb[:, g * 256:(g + 1) * 256], r=[ct, Hsb], w=[bko])
        P.tt('dve', v3(t1[:], a=8), v3(bko[:, 0:512], a=8), bc(dts[:, 4, :], [128, 8, 64], 2), ALU.mult,
             r=[bko, dts], w=[t1])
        P.tt('dve', t1[:], t1[:], bky[:, 0:512], ALU.add, r=[t1, bky], w=[t1])
        P.tt('pool', v3(t3[:], a=8), v3(xsB[:, 0:512], a=8), bc(pA[:, o_dsk:o_dsk + 8], [128, 8, 64], 2), ALU.mult,
             r=[xsB, pA], w=[t3])
        P.tt('pool', t3[:], t3[:], t1[:], ALU.add, r=[t3, t1], w=[t3])
        P.tt('pool', ygl[:], t3[:], szb[:], ALU.mult, r=[t3, szb], w=[ygl])
        P.dma('sp', sc['ygl'][tsl, :], ygl[:], r=[ygl])
        P.tt('dve', dts[:, 6, :], dts[:, 4, :], Pc[:], ALU.mult, r=[dts, Pc], w=[dts])
        P.dma('sp', sc['eat'][tsl, :], dts[:, 6, :], r=[dts])
        P.tt('dve', Pc[:], Pc[:], dts[:, 5, :], ALU.mult, r=[Pc, dts], w=[Pc])
        bks = pb(c)
        for g in range(2):
            P.mm(bks[:, g * 256:(g + 1) * 256], xsB[:, 512 + g * 128:512 + (g + 1) * 128], xdt2[:, g * 256:(g + 1) * 256],
                 r=[xsB, xdt2], w=[bks])
        P.tt('dve', v3(Hs32[:], a=8), v3(Hs32[:], a=8), bc(dts[:, 5, :], [128, 8, 64], 2), ALU.mult,
             r=[Hs32, dts], w=[Hs32])
        P.tt('dve', Hs32[:], Hs32[:], bks[:, 0:512], ALU.add, r=[Hs32, bks], w=[Hs32])
        P.cp('pool', Hsb[:], Hs32[:], r=[Hs32], w=[Hsb])

        c.ck('ssd')
        for hp in range(2):
            bk = pb(c)
            for hh in range(2):
                p = slice(hh * 64, (hh + 1) * 64)
                P.mm(bk[:, hh * 256:(hh + 1) * 256], qTb[p, hp, :], c.KT[p, hp, :], r=[qTb, c.KT], w=[bk], rg=hh * 64)
            P.op('dve', lambda e: e.reduce_max(xs4[:, 0, hp * 2:hp * 2 + 2], v3(bk[:, 0:512], a=2), AX.X), r=[bk], w=[xs4])
            c.ck('x1')
            P.ts('dve', xs4[:, 1, hp * 2:hp * 2 + 2], xs4[:, 0, hp * 2:hp * 2 + 2], -0.125, ALU.mult, r=[xs4], w=[xs4])
            c.ck('x2')
            for hh in range(2):
                h = hp * 2 + hh
                P.act(Pb[:, h, :], bk[:, hh * 256:(hh + 1) * 256], AF.Exp, r=[bk, xs4], w=[Pb, xs4],
                      bias=xs4[:, 1, h:h + 1], scale=0.125, accum=xs4[:, 2, h:h + 1])
        c.ck('x3')
        bk = pb(c)
        pbf = bk[:].bitcast(BF16)
        for h in range(4):
            for mt_ in range(2):
                j = h * 2 + mt_
                P.tr(pbf[:, j * 128:(j + 1) * 128], Pb[:, h, mt_ * 128:(mt_ + 1) * 128], cb[:, BI, :], r=[Pb, cb], w=[bk])
        P.cp('act', PTb[:], v3(pbf[:, 0:1024], a=8), r=[bk], w=[PTb])
        bk = pb(c)
        for h in range(4):
            for mt_ in range(2):
                P.mm(bk[:, h * 64:(h + 1) * 64], PTb[:, h * 2 + mt_, :], c.Vm[:, mt_, h * 64:(h + 1) * 64],
                     start=(mt_ == 0), stop=(mt_ == 1), r=[PTb, c.Vm], w=[bk])
        c.ck('x4')
        P.op('dve', lambda e: e.reciprocal(xs4[:, 3, :], xs4[:, 2, :]), r=[xs4], w=[xs4])
        P.tt('dve', v3(ym[:], a=4), v3(bk[:, 0:256], a=4), bc(xs4[:, 3, :], [128, 4, 64], 2), ALU.mult, r=[bk, xs4], w=[ym])
        P.tt('pool', ymb[:], ym[:], sgm[:], ALU.mult, r=[ym, sgm], w=[ymb])
        P.dma('sp', sc['ym'][tsl, :], ymb[:], r=[ymb])

        c.ck('xattn')
        for bi, (c0, c1) in enumerate(((0, 512), (512, 1024), (1024, 1152))):
            bk = pb(c)
            P.mm(bk[:, 0:c1 - c0], cb[:, BSH, :], ubc[:, c0:c1], start=True, stop=False, r=[cb, ubc], w=[bk])
            if i == 0:
                P.mm(bk[:, 0:c1 - c0], cb[0:3, BE0, :], ubh[0:3, c0:c1], start=False, stop=True, r=[cb, ubh], w=[bk])
            else:
                P.mm(bk[:, 0:c1 - c0], cb[:, BE, :], ubp[:, c0:c1], start=False, stop=True, r=[cb, ubp], w=[bk])
            P.tt('dve', us[:, c0:c1], bk[:, 0:c1 - c0], u32[:, c0:c1], ALU.subtract, r=[bk, u32], w=[us])
        P.tt('pool', us[:], us[:], pA[:, o_mu:o_mu + 1152], ALU.mult, r=[us, pA], w=[us])
        P.tt('pool', us[:], us[:], u32[:], ALU.add, r=[us, u32], w=[us])
        r_, k_, v_, g_ = us[:, 0:256], us[:, 256:512], us[:, 512:768], us[:, 768:1024]
        P.act(latb[:, 0:64], us[:, 1024:1088], AF.Tanh, r=[us], w=[latb])
        P.cp('pool', latb[:, 64:128], us[:, 1088:1152], r=[us], w=[latb])
        bk = pb(c)
        pbf = bk[:].bitcast(BF16)
        P.tr(pbf[:, 0:128], latb[:], cb[:, BI, :], r=[latb, cb], w=[bk])
        P.cp('act', latT[:], pbf[:, 0:128], r=[bk], w=[latT])
        bk = pb(c)
        P.mm(bk[:, 0:256], latT[0:64, :], w2a2[0:64, :], r=[latT, w2a2], w=[bk], rg=0)
        P.mm(bk[:, 256:512], latT[64:128, :], w2a2[64:128, :], r=[latT, w2a2], w=[bk], rg=64)
        P.tt('dve', pre[:], bk[:, 0:512], pA[:, o_w0a0:o_w0a0 + 512], ALU.add, r=[bk, pA], w=[pre])
        P.act(sig[:], pre[:], AF.Sigmoid, r=[pre], w=[sig])
        sw, a_ = sig[:, 0:256], sig[:, 256:512]
        P.tt('pool', kk0[:], k_, pA[:, o_kk:o_kk + 256], ALU.mult, r=[us, pA], w=[kk0])
        P.tt('pool', kk2[:], kk0[:], kk0[:], ALU.mult, r=[kk0], w=[kk2])
        P.op('dve', lambda e: e.reduce_sum(sm4[:, 0, :], v3(kk2[:], a=4), AX.X), r=[kk2], w=[sm4])
        P.act(sm4[:, 0, :], sm4[:, 0, :], AF.Sqrt, r=[sm4], w=[sm4])
        P.ts('dve', sm4[:, 0, :], sm4[:, 0, :], 1e-12, ALU.max, r=[sm4], w=[sm4])
        P.op('dve', lambda e: e.reciprocal(sm4[:, 1, :], sm4[:, 0, :]), r=[sm4], w=[sm4])
        P.tt('dve', v3(kk[:], a=4), v3(kk0[:], a=4), bc(sm4[:, 1, :], [128, 4, 64], 2), ALU.mult, r=[kk0, sm4], w=[kk])
        P.stt('dve', k2[:], a_, -1.0, pA[:, o_ka:o_ka + 256], ALU.add, ALU.mult, r=[sig, pA], w=[k2])
        P.stt('dve', k2[:], k2[:], 1.0, k_, ALU.add, ALU.mult, r=[k2, us], w=[k2])
        P.tt('pool', bb[:], kk[:], a_, ALU.mult, r=[kk, sig], w=[bb])
        bk = pb(c)
        P.mm(bk[:, 0:256], cf[:, CTRIU, :], sw, r=[cf, sig], w=[bk])
        P.mm(bk[:, 256:512], cf[:, CONES, :], sw, r=[cf, sig], w=[bk])
        P.cp('act', cst[:], bk[:, 0:512], r=[bk], w=[cst])
        bkg = pb(c)
        for ft in range(2):
            P.mm(bkg[:, ft:ft + 1], sig[:, ft * 128:(ft + 1) * 128], cf[:, CONES, 0:1], r=[cf, sig], w=[bkg])
        P.act(gC[:], bkg[:, 0:2], AF.Exp, r=[bkg], w=[gC], scale=LW)
        P.tt('pool', dd[:, 0:256], cst[:, 256:512], cst[:, 0:256], ALU.subtract, r=[cst], w=[dd])
        P.tt('pool', dd[:, 256:512], cst[:, 0:256], sw, ALU.subtract, r=[cst, sig], w=[dd])
        P.act(g4[:, 0, :], cst[:, 0:256], AF.Exp, r=[cst], w=[g4], scale=LW)
        P.act(g4[:, 2, :], cst[:, 0:256], AF.Exp, r=[cst], w=[g4], scale=-LW)
        P.act(g4[:, 1, :], dd[:, 0:256], AF.Exp, r=[dd], w=[g4], scale=LW)
        P.act(g4[:, 3, :], dd[:, 256:512], AF.Exp, r=[dd], w=[g4], scale=LW)
        P.tt('dve', tok4[:, 0, :], bb[:], g4[:, 2, :], ALU.mult, r=[bb, g4], w=[tok4])
        P.tt('dve', tok4[:, 1, :], k2[:], g4[:, 2, :], ALU.mult, r=[k2, g4], w=[tok4])
        P.stt('dve', tok4[:, 2, :], kk[:], -1.0, g4[:, 3, :], ALU.mult, ALU.mult, r=[kk, g4], w=[tok4])
        P.tt('pool', tok4[:, 3, :], r_, g4[:, 0, :], ALU.mult, r=[us, g4], w=[tok4])
        P.cp('pool', zview(az), v3(tok4[:, 2, :], a=2).rearrange("p f (h d) -> p f h d", h=2), r=[tok4], w=[az])
        P.tt('dve', zview(Bz), v3(bb[:], a=2).rearrange("p f (h d) -> p f h d", h=2),
             v3(g4[:, 1, :], a=2).rearrange("p f (h d) -> p f h d", h=2), ALU.mult, r=[bb, g4], w=[Bz])
        P.tt('pool', zview(Kz), v3(k2[:], a=2).rearrange("p f (h d) -> p f h d", h=2),
             v3(g4[:, 1, :], a=2).rearrange("p f (h d) -> p f h d", h=2), ALU.mult, r=[k2, g4], w=[Kz])
        P.cp('act', Vb[:], v_, r=[us], w=[Vb])
        P.tt('pool', rk[:], r_, k2[:], ALU.mult, r=[us, k2], w=[rk])
        P.tt('pool', rk[:], rk[:], pA[:, o_rk:o_rk + 256], ALU.mult, r=[rk, pA], w=[rk])
        P.op('dve', lambda e: e.reduce_sum(sm4[:, 2, :], v3(rk[:], a=4), AX.X), r=[rk], w=[sm4])
        P.tt('dve', v3(rk[:], a=4), v3(v_, a=4), bc(sm4[:, 2, :], [128, 4, 64], 2), ALU.mult, r=[us, sm4], w=[rk])
        P.act(sgf[:], g_, AF.Silu, r=[us], w=[sgf])
        P.cp('pool', sgb[:], sgf[:], r=[sgf], w=[sgb])
        P.tt('pool', bsg[:], rk[:], sgf[:], ALU.mult, r=[rk, sgf], w=[bsg])
        P.dma('sp', sc['sg'][tsl, :], sgb[:], r=[sgb])
        P.dma('sp', sc['bsg'][tsl, :], bsg[:], r=[bsg])
        c.ck('rwkv_tok')
        bk = pb(c)
        pbf = bk[:].bitcast(BF16)
        for ft in range(2):
            for wi in range(4):
                j = ft * 4 + wi
                P.tr(pbf[:, j * 128:(j + 1) * 128], tok4[:, wi, ft * 128:(ft + 1) * 128], cb[:, BI, :], r=[tok4, cb], w=[bk])
        P.cp('act', FT[:], pbf[:, 0:1024].rearrange("p (f w t) -> p f w t", f=2, w=4), r=[bk], w=[FT])
        for hp in range(2):
            bk1, bk2 = pb(c), pb(c)
            for hh in range(2):
                p = slice(hh * 64, (hh + 1) * 64)
                P.mm(bk1[:, hh * 256:(hh + 1) * 256], FT[p, hp, 2, :], FT[p, hp, 0:2, :], r=[FT], w=[bk1], rg=hh * 64)
                P.mm(bk2[:, hh * 256:(hh + 1) * 256], FT[p, hp, 0, :], FT[p, hp, 2:4, :], r=[FT], w=[bk2], rg=hh * 64)
            P.tt('dve', Xm1[:, hp * 2:hp * 2 + 2, :, :], bk1[:, 0:512].rearrange("p (h w s) -> p h w s", h=2, w=2),
                 bc2(cf[:, CLS, :], [128, 2, 2, 128]), ALU.mult, r=[bk1, cf], w=[Xm1])
            P.tt('dve', Xm2[:, hp * 2:hp * 2 + 2, :, :], bk2[:, 0:512].rearrange("p (h w s) -> p h w s", h=2, w=2),
                 bc(m2[:], [128, 2, 2, 128], 1), ALU.mult, r=[bk2, m2], w=[Xm2])
        bk = pb(c)
        for h in range(4):
            hp, hh = h // 2, h % 2
            p = slice(hh * 64, (hh + 1) * 64)
            P.mm(bk[:, h * 128:(h + 1) * 128], FT[p, hp, 1, :], FT[p, hp, 3, :], r=[FT], w=[bk], rg=hh * 64)
        P.tt('dve', ArkT[:], v3(bk[:, 0:512], a=4), bc(cf[:, CTRIU, :], [128, 4, 128], 1), ALU.mult, r=[bk, cf], w=[ArkT])
        c.ck('rwkv_x')
        P.cp('pool', Pp[0][:], Xm1[:, :, 0, :], r=[Xm1], w=[Pp[0]])
        P.cp('pool', PTp[0][:], Xm2[:, :, 0, :], r=[Xm2], w=[PTp[0]])
        P.tt('dve', TT32[:], Xm2[:, :, 0, :], bc(cf[:, CI, :], [128, 4, 128], 1), ALU.add, r=[Xm2, cf], w=[TT32])
        P.cp('pool', TTb[0][:], TT32[:], r=[TT32], w=[TTb[0]])
        cur = 0
        for kq in range(1, 7):
            nx = 1 - cur
            bkP = pb(c)
            for h in range(4):
                P.mm(bkP[:, h * 128:(h + 1) * 128], PTp[cur][:, h, :], Pp[cur][:, h, :], r=[PTp[cur], Pp[cur]], w=[bkP])
            if kq < 6:
                bkT = pb(c)
                for h in range(4):
                    P.mm(bkT[:, h * 128:(h + 1) * 128], Pp[cur][:, h, :], PTp[cur][:, h, :], r=[PTp[cur], Pp[cur]], w=[bkT])
            P.cp('act', Pp[nx][:], v3(bkP[:, 0:512], a=4), r=[bkP], w=[Pp[nx]])
            if kq < 6:
                P.cp('dve', PTp[nx][:], v3(bkT[:, 0:512], a=4), r=[bkT], w=[PTp[nx]])
            bkU = pb(c)
            for h in range(4):
                P.mm(bkU[:, h * 128:(h + 1) * 128], Pp[nx][:, h, :], TTb[cur][:, h, :], r=[Pp[nx], TTb[cur]], w=[bkU])
            P.tt('dve', TT32[:], TT32[:], v3(bkU[:, 0:512], a=4), ALU.add, r=[TT32, bkU], w=[TT32])
            P.cp('pool', TTb[nx][:], TT32[:], r=[TT32], w=[TTb[nx]])
            cur = nx
        TTf = TTb[cur]
        c.ck('rwkv_inv')
        bk = pb(c)
        for hp in range(2):
            for hh in range(2):
                h = hp * 2 + hh
                P.mm(bk[:, hp * 128:(hp + 1) * 128], az[:, h, :], TTf[:, h, :], start=(hh == 0), stop=(hh == 1),
                     r=[az, TTf], w=[bk])
        P.cp('act', WT[:], v3(bk[:, 0:256], a=2), r=[bk], w=[WT])
        bk = pb(c)
        for h in range(4):
            P.mm(bk[:, h * 128:(h + 1) * 128], Xm1[:, h, 1, :], TTf[:, h, :], r=[Xm1, TTf], w=[bk])
        P.cp('dve', MT[:], v3(bk[:, 0:512], a=4), r=[bk], w=[MT])
        c.ck('rwkv_wm')
        bkU = pb(c)
        for h in range(4):
            hp, hh = h // 2, h % 2
            p = slice(hh * 64, (hh + 1) * 64)
            P.mm(bkU[:, h * 128:(h + 1) * 128], WT[p, hp, :], Hb[p, hp, :], start=True, stop=False, r=[WT, Hb], w=[bkU], rg=hh * 64)
            P.mm(bkU[:, h * 128:h * 128 + 64], MT[:, h, :], Vb[:, h * 64:(h + 1) * 64], start=False, stop=True,
                 r=[MT, Vb], w=[bkU])
        P.cp('act', Ub[:], v3(bkU[:, 0:512], a=4), r=[bkU], w=[Ub])
        bkY = pb(c)
        for h in range(4):
            hp, hh = h // 2, h % 2
            p = slice(hh * 64, (hh + 1) * 64)
            P.mm(bkY[:, h * 128:(h + 1) * 128], Hb[p, hp, :], FT[p, hp, 3, :], start=True, stop=False, r=[Hb, FT], w=[bkY], rg=hh * 64)
            P.mm(bkY[:, h * 128:(h + 1) * 128], Ub[:, h, :], Xm2[:, h, 1, :], start=False, stop=False, r=[Ub, Xm2], w=[bkY])
            P.mm(bkY[0:64, h * 128:(h + 1) * 128], Vb[:, h * 64:(h + 1) * 64], ArkT[:, h, :], start=False, stop=True,
                 r=[Vb, ArkT], w=[bkY])
        P.cp('act', YTb[:], v3(bkY[:, 0:512], a=4), r=[bkY], w=[YTb])
        P.dma('sp', sc['yt'][:, i, :, :], YTb[:], r=[YTb])
        bkH = pb(c)
        for hp in range(2):
            for hh in range(2):
                h = hp * 2 + hh
                P.mm(bkH[:, hp * 128:(hp + 1) * 128], Bz[:, h, :], Ub[:, h, :], start=(hh == 0), stop=False, r=[Bz, Ub], w=[bkH])
            for hh in range(2):
                h = hp * 2 + hh
                P.mm(bkH[:, hp * 128:hp * 128 + 64], Kz[:, h, :], Vb[:, h * 64:(h + 1) * 64], start=False, stop=(hh == 1),
                     r=[Kz, Vb], w=[bkH])
        for hp in range(2):
            P.stt('dve', H32[:, hp, :], H32[:, hp, :], gC[:, hp:hp + 1], bkH[:, hp * 128:(hp + 1) * 128], ALU.mult, ALU.add,
                  r=[H32, gC, bkH], w=[H32])
        P.cp('pool', Hb[:], H32[:], r=[H32], w=[Hb])

    P.dma('sp', sc['st_hs'], Hs32[:], r=[Hs32])
    P.dma('sp', sc['st_ds'], Pc[:], r=[Pc])
    P.dma('sp', sc['st_hr'], H32[:], r=[H32])


def bc2(ap, shape):
    return ap.unsqueeze(1).unsqueeze(1).to_broadcast(list(shape))


def phase_B(c, L, x_tile, x_out, sc, G, pm_d, es):
    P, NT = c.P, c.NT
    cf, cb, cfb = c.cf, c.cb, c.cfb
    NC = c.ncores
    S = lambda n, s, d: sb(c, "B_" + n, s, d, es)
    w_out = S("w_out", [128, 8, 1024], BF16)
    for kc in range(8):
        P.dma('pool', w_out[:, kc, :], L['w_out'][kc * 128:(kc + 1) * 128, :], w=[w_out])
    pB = S("pB", [128, 2048], F32)
    o_pw, o_nw, o_lw, o_lb = 0, 1024, 1536, 1792
    P.dma('sp', pB[:, o_pw:o_pw + 1024], L['post_norm_w'].partition_broadcast(128), w=[pB])
    P.dma('sp', pB[:, o_nw:o_nw + 512], L['ssm_norm_w'].partition_broadcast(128), w=[pB])
    P.dma('sp', pB[:, o_lw:o_lw + 256], L['lnx_w'].partition_broadcast(128), w=[pB])
    P.dma('sp', pB[:, o_lb:o_lb + 256], L['lnx_b'].partition_broadcast(128), w=[pB])
    pm = S("pm", [128, NC], F32)
    P.dma('sp', pm[:], pm_d, w=[pm])

    Hin = S("Hin", [128, 512], F32)
    Hinb = S("Hinb", [128, 512], BF16)
    P.memset('pool', Hin[:], 0.0, w=[Hin])
    hs = [S("hs%d" % i, [128, 512], F32) for i in range(2)]
    ds = [S("ds%d" % i, [128, 8], F32) for i in range(2)]
    dd_ = S("ddm", [128, 8], F32)
    for cp_ in range(NC - 1):
        hsc, dsc = hs[cp_ % 2], ds[cp_ % 2]
        P.dma('sp', hsc[:], G['hs'][cp_], w=[hsc])
        P.dma('sp', dsc[:], G['ds'][cp_], w=[dsc])
        P.stt('dve', dd_[:], dsc[:], -1.0, pm[:, cp_:cp_ + 1].to_broadcast([128, 8]), ALU.add, ALU.mult, r=[dsc, pm], w=[dd_])
        P.ts('dve', dd_[:], dd_[:], 1.0, ALU.add, r=[dd_], w=[dd_])
        P.tt('dve', v3(Hin[:], a=8), v3(Hin[:], a=8), bc(dd_[:], [128, 8, 64], 2), ALU.mult, r=[Hin, dd_], w=[Hin])
        P.stt('dve', Hin[:], hsc[:], pm[:, cp_:cp_ + 1], Hin[:], ALU.mult, ALU.add, r=[hsc, pm, Hin], w=[Hin])
    P.cp('act', Hinb[:], Hin[:], r=[Hin], w=[Hinb])

    HB = S("HB", [128, 4, 64], F32)
    HBb = S("HBb", [128, 4, 64], BF16)
    P.memset('pool', HB[:], 0.0, w=[HB])
    for h in range(4):
        P.cp('dve', HB[0:64, h, :], cf[0:64, CI, 0:64], r=[cf], w=[HB])
    Pz = [S("Pz%d" % i, [128, 4, 128], F32) for i in range(2)]
    Hl = [S("Hl%d" % i, [128, 4, 64], F32) for i in range(2)]
    PzT = S("PzT", [128, 4, 128], F32)
    dlt = S("dlt", [128, 4, 64], F32)
    for z in Pz:
        P.memset('pool', z[:], 0.0, w=[z])
    for cp_ in range(NC - 1):
        pz, hl = Pz[cp_ % 2], Hl[cp_ % 2]
        for h in range(4):
            hp, hh = h // 2, h % 2
            P.dma('sp', pz[64:128, h, 64:128], G['hr'][cp_, hh * 64:(hh + 1) * 64, hp, 64:128], w=[pz])
            P.dma('sp', hl[64:128, h, :], G['hr'][cp_, hh * 64:(hh + 1) * 64, hp, 0:64], w=[hl])
        bk = pb(c)
        for h in range(4):
            P.tr(bk[:, h * 128:(h + 1) * 128], pz[:, h, :], cf[:, CI, :], r=[pz, cf], w=[bk])
        P.cp('act', PzT[:], v3(bk[:, 0:512], a=4), r=[bk], w=[PzT])
        bk = pb(c)
        for h in range(4):
            P.mm(bk[:, h * 64:(h + 1) * 64], PzT[:, h, :], HB[:, h, :], r=[PzT, HB], w=[bk])
        P.tt('dve', dlt[64:128], v3(bk[64:128, 0:256], a=4), hl[64:128], ALU.add, r=[bk, hl], w=[dlt])
        P.tt('dve', dlt[64:128], dlt[64:128], HB[64:128], ALU.subtract, r=[dlt, HB], w=[dlt])
        P.stt('dve', HB[64:128], dlt[64:128], pm[64:128, cp_:cp_ + 1], HB[64:128], ALU.mult, ALU.add,
              r=[dlt, pm, HB], w=[HB])
    P.cp('act', HBb[:], HB[:], r=[HB], w=[HBb])

    ygl = [S("ygl%d" % i, [128, 512], BF16) for i in range(2)]
    szb = [S("szb%d" % i, [128, 512], BF16) for i in range(2)]
    ctb = [S("ctb%d" % i, [128, 2, 128], BF16) for i in range(2)]
    ytb = [S("ytb%d" % i, [128, 4, 128], BF16) for i in range(2)]
    bsg = [S("bsg%d" % i, [128, 256], BF16) for i in range(2)]
    sgb = [S("sgb%d" % i, [128, 256], BF16) for i in range(2)]
    eat = [S("eat%d" % i, [128, 8], F32) for i in range(2)]
    ycat = [S("ycat%d" % i, [128, 1024], BF16) for i in range(2)]
    yc = S("yc", [128, 512], F32)
    yg = S("yg", [128, 512], F32)
    junk = S("junk", [128, 1024], F32)
    st = S("st", [128, 8, 4], F32)
    yr = S("yr", [128, 256], F32)
    yr2 = S("yr2", [128, 256], F32)
    yT = S("yT", [128, 8, 128], BF16)
    ot = S("ot", [128, 1024], F32)
    for i in range(NT):
        tsl = slice(i * 128, (i + 1) * 128)
        q = i % 2
        P.dma('sp', ygl[q][:], sc['ygl'][tsl, :], w=[ygl[q]])
        P.dma('sp', szb[q][:], sc['sz'][tsl, :], w=[szb[q]])
        P.dma('sp', ctb[q][:], sc['ct'][:, i, :, :], w=[ctb[q]])
        P.dma('sp', ytb[q][:], sc['yt'][:, i, :, :], w=[ytb[q]])
        P.dma('sp', bsg[q][:], sc['bsg'][tsl, :], w=[bsg[q]])
        P.dma('sp', sgb[q][:], sc['sg'][tsl, :], w=[sgb[q]])
        P.dma('sp', eat[q][:], sc['eat'][tsl, :], w=[eat[q]])
        P.dma('sp', ycat[q][:, 768:1024], sc['ym'][tsl, :], w=[ycat[q]])
        bk = pb(c)
        for g in range(2):
            P.mm(bk[:, g * 256:(g + 1) * 256], ctb[q][:, g, :], Hinb[:, g * 256:(g + 1) * 256], r=[ctb[q], Hinb], w=[bk])
        P.tt('dve', v3(yc[:], a=8), v3(bk[:, 0:512], a=8), bc(eat[q][:], [128, 8, 64], 2), ALU.mult, r=[bk, eat[q]], w=[yc])
        P.tt('pool', yc[:], yc[:], szb[q][:], ALU.mult, r=[yc, szb[q]], w=[yc])
        P.tt('pool', yg[:], yc[:], ygl[q][:], ALU.add, r=[yc, ygl[q]], w=[yg])
        for g in range(2):
            P.act(junk[:, g * 256:(g + 1) * 256], yg[:, g * 256:(g + 1) * 256], AF.Square, r=[yg], w=[junk, st],
                  accum=st[:, g, 0:1])
        P.ts('dve', st[:, 0:2, 1], st[:, 0:2, 0], 1.0 / 256, ALU.mult, NORM_EPS, ALU.add, r=[st], w=[st])
        P.act(st[:, 0:2, 1], st[:, 0:2, 1], AF.Sqrt, r=[st], w=[st])
        P.op('dve', lambda e: e.reciprocal(st[:, 0:2, 2], st[:, 0:2, 1]), r=[st], w=[st])
        P.tt('dve', v3(yg[:], a=2), v3(yg[:], a=2), bc(st[:, 0:2, 2], [128, 2, 256], 2), ALU.mult, r=[yg, st], w=[yg])
        P.tt('pool', ycat[q][:, 0:512], yg[:], pB[:, o_nw:o_nw + 512], ALU.mult, r=[yg, pB], w=[ycat[q]])
        bk = pb(c)
        for h in range(4):
            P.mm(bk[:, h * 64:(h + 1) * 64], ytb[q][:, h, :], HBb[:, h, :], r=[ytb[q], HBb], w=[bk])
        P.op('dve', lambda e: e.reduce_sum(st[:, 4:8, 0], v3(bk[:, 0:256], a=4), AX.X), r=[bk], w=[st])
        P.ts('dve', st[:, 4:8, 0], st[:, 4:8, 0], -1.0 / 64, ALU.mult, r=[st], w=[st])
        P.tt('dve', v3(yr[:], a=4), v3(bk[:, 0:256], a=4), bc(st[:, 4:8, 0], [128, 4, 64], 2), ALU.add, r=[bk, st], w=[yr])
        P.tt('pool', yr2[:], yr[:], yr[:], ALU.mult, r=[yr], w=[yr2])
        P.op('dve', lambda e: e.reduce_sum(st[:, 4:8, 1], v3(yr2[:], a=4), AX.X), r=[yr2], w=[st])
        P.ts('dve', st[:, 4:8, 1], st[:, 4:8, 1], 1.0 / 64, ALU.mult, LNX_EPS, ALU.add, r=[st], w=[st])
        P.act(st[:, 4:8, 1], st[:, 4:8, 1], AF.Sqrt, r=[st], w=[st])
        P.op('dve', lambda e: e.reciprocal(st[:, 4:8, 2], st[:, 4:8, 1]), r=[st], w=[st])
        P.tt('dve', v3(yr[:], a=4), v3(yr[:], a=4), bc(st[:, 4:8, 2], [128, 4, 64], 2), ALU.mult, r=[yr, st], w=[yr])
        P.tt('pool', yr[:], yr[:], pB[:, o_lw:o_lw + 256], ALU.mult, r=[yr, pB], w=[yr])
        P.tt('pool', yr[:], yr[:], pB[:, o_lb:o_lb + 256], ALU.add, r=[yr, pB], w=[yr])
        P.tt('pool', yr[:], yr[:], sgb[q][:], ALU.mult, r=[yr, sgb[q]], w=[yr])
        P.tt('pool', ycat[q][:, 512:768], yr[:], bsg[q][:], ALU.add, r=[yr, bsg[q]], w=[ycat[q]])
        bk = pb(c)
        pbf = bk[:].bitcast(BF16)
        for kc in range(8):
            P.tr(pbf[:, kc * 128:(kc + 1) * 128], ycat[q][:, kc * 128:(kc + 1) * 128], cb[:, BI, :], r=[ycat[q], cb], w=[bk])
        P.cp('act', yT[:], v3(pbf[:, 0:1024], a=8), r=[bk], w=[yT])
        bko = [pb(c), pb(c)]
        for n in range(2):
            for kc in range(8):
                P.mm(bko[n][:, 0:512], yT[:, kc, :], w_out[:, kc, n * 512:(n + 1) * 512], start=(kc == 0), stop=(kc == 7),
                     r=[yT, w_out], w=[bko[n]])
            P.act(junk[:, n * 512:(n + 1) * 512], bko[n][:, 0:512], AF.Square, r=[bko[n]], w=[junk, st],
                  accum=st[:, 2 + n, 0:1])
        P.tt('dve', st[:, 2, 1:2], st[:, 2, 0:1], st[:, 3, 0:1], ALU.add, r=[st], w=[st])
        P.ts('dve', st[:, 2, 1:2], st[:, 2, 1:2], 1.0 / 1024, ALU.mult, NORM_EPS, ALU.add, r=[st], w=[st])
        P.act(st[:, 2, 1:2], st[:, 2, 1:2], AF.Sqrt, r=[st], w=[st])
        P.op('dve', lambda e: e.reciprocal(st[:, 2, 2:3], st[:, 2, 1:2]), r=[st], w=[st])
        for n in range(2):
            P.stt('dve', ot[:, n * 512:(n + 1) * 512], bko[n][:, 0:512], st[:, 2, 2:3], pB[:, o_pw + n * 512:o_pw + (n + 1) * 512],
                  ALU.mult, ALU.mult, r=[bko[n], st, pB], w=[ot])
        xt, xkeys = x_tile(i)
        P.tt('pool', xt, xt, ot[:], ALU.add, r=xkeys + [ot], w=xkeys)
        x_out(i, xt, xkeys)


LAYER_A = [("pre_norm_w", [D]), ("w_in", [D, IN_W]), ("conv_w", [4, D]), ("conv_b", [D]), ("dt_bias", [8]),
           ("a_log", [8]), ("d_skip", [8]), ("shift_mu", [1152]), ("w0", [256]), ("w2", [64, 256]), ("a0", [256]),
           ("a2", [64, 256]), ("k_k", [256]), ("k_a", [256]), ("r_k", [4, 64])]
LAYER_B = [("w_out", [D, D]), ("post_norm_w", [D]), ("ssm_norm_w", [512]), ("lnx_w", [256]), ("lnx_b", [256])]


def scratch_specs(NT):
    T = NT * 128
    return [("sz", [T, 512], BF16), ("ygl", [T, 512], BF16), ("ct", [128, NT, 2, 128], BF16),
            ("yt", [128, NT, 4, 128], BF16), ("bsg", [T, 256], BF16), ("sg", [T, 256], BF16),
            ("ym", [T, 256], BF16), ("eat", [T, 8], F32)]


STATE_SPECS = [("st_hs", [128, 512]), ("st_ds", [128, 8]), ("st_hr", [128, 2, 128])]


def build_A(NT, ncores):
    nc = bass.Bass("TRN2", target_bir_lowering=False)
    T = NT * 128
    di = lambda n, s, d=F32: nc.dram_tensor(n, s, d, kind="ExternalInput").ap()
    do = lambda n, s, d=F32: nc.dram_tensor(n, s, d, kind="ExternalOutput").ap()
    x_d, xh_d = di("x", [T, D]), di("xh", [3, D])
    cf_d, cb_d = di("cf", [128, NCF, 128]), di("cb", [128, NCB, 128])
    mem_d, mnw_d, wkv_d = di("mem", [256, D]), di("mem_norm_w", [D]), di("w_mem_kv", [D, 512])
    L = {n: di(n, s) for n, s in LAYER_A}
    sc = {n: do(n, s, d) for n, s, d in scratch_specs(NT)}
    for n, s in STATE_SPECS:
        sc[n] = do(n, s)
    with ExitStack() as es:
        c = make_ctx(nc, es, NT)
        c.ncores = ncores
        P = c.P
        load_consts(c, cf_d, cb_d)
        c.w_in_sb = sb(c, "w_in_sb", [128, 8, IN_W], BF16)
        with ExitStack() as es0:
            mem_kv(c, mem_d, mnw_d, wkv_d, es0)
            P.barrier()
        xh = sb(c, "xh_sb", [3, D], F32)
        P.dma('sp', xh[:], xh_d, w=[xh])
        xts = [sb(c, "xt%d" % i, [128, D], F32) for i in range(2)]

        def x_tile(i):
            b = xts[i % 2]
            P.dma('sp', b[:], x_d[i * 128:(i + 1) * 128, :], w=[b])
            return b[:], [b]
        import os
        c.stop = os.environ.get("KSTOP")
        with ExitStack() as esA:
            try:
                if c.stop != 'memkv':
                    phase_A(c, L, x_tile, (xh[0:3, :], [xh]), sc, esA)
            except StopBuild:
                pass
            P.barrier()
        print("A program instructions:", P.ninst, dict(P.cnt))
    return nc


def build_B(NT, ncores):
    nc = bass.Bass("TRN2", target_bir_lowering=False)
    T = NT * 128
    di = lambda n, s, d=F32: nc.dram_tensor(n, s, d, kind="ExternalInput").ap()
    do = lambda n, s, d=F32: nc.dram_tensor(n, s, d, kind="ExternalOutput").ap()
    x_d = di("x", [T, D])
    xo_d = do("xo", [T, D])
    cf_d, cb_d = di("cf", [128, NCF, 128]), di("cb", [128, NCB, 128])
    L = {n: di(n, s) for n, s in LAYER_B}
    sc = {n: di(n, s, d) for n, s, d in scratch_specs(NT)}
    G = {"hs": di("g_hs", [ncores, 128, 512]), "ds": di("g_ds", [ncores, 128, 8]), "hr": di("g_hr", [ncores, 128, 2, 128])}
    pm_d = di("pm", [128, ncores])
    with ExitStack() as es:
        c = make_ctx(nc, es, NT)
        c.ncores = ncores
        P = c.P
        load_consts(c, cf_d, cb_d)
        xts = [sb(c, "xt%d" % i, [128, D], F32) for i in range(2)]

        def x_tile(i):
            b = xts[i % 2]
            P.dma('sp', b[:], x_d[i * 128:(i + 1) * 128, :], w=[b])
            return b[:], [b]

        def x_out(i, ap, keys):
            P.dma('sp', xo_d[i * 128:(i + 1) * 128, :], ap, r=keys)
        with ExitStack() as esB:
            phase_B(c, L, x_tile, x_out, sc, G, pm_d, esB)
            P.barrier()
    return nc


_CACHE = {}


def _get(kind, NT, ncores):
    k = (kind, NT, ncores)
    if k not in _CACHE:
        _CACHE[k] = (build_A if kind == 'A' else build_B)(NT, ncores)
    return _CACHE[k]


def run_unfused(inputs, NT, ncores, depth):
    f32 = lambda a: np.ascontiguousarray(np.asarray(a, dtype=np.float32))
    T = NT * 128
    x = f32(inputs["x"])[0]
    cf, cb = host_consts()
    ids = list(range(ncores))
    pms = [np.ascontiguousarray(np.broadcast_to((np.arange(ncores) < ci).astype(np.float32)[None, :], (128, ncores)))
           for ci in ids]
    com = {"cf": cf, "cb": cb}
    ncA, ncB = _get('A', NT, ncores), _get('B', NT, ncores)
    for l in range(depth):
        la = {n: f32(inputs[n][l]) for n, _ in LAYER_A}
        lb = {n: f32(inputs[n][l]) for n, _ in LAYER_B}
        mapsA = []
        for ci in ids:
            s = ci * T
            xh = x[s - 3:s] if ci > 0 else np.zeros((3, D), np.float32)
            m = dict(com)
            m.update(la)
            m.update({"x": x[s:s + T], "xh": np.ascontiguousarray(xh), "mem": f32(inputs["mem"])[0],
                      "mem_norm_w": f32(inputs["mem_norm_w"]), "w_mem_kv": f32(inputs["w_mem_kv"])})
            mapsA.append(m)
        ra = run_bass_kernel_spmd(ncA, mapsA, core_ids=ids).results
        g_hs = np.stack([ra[ci]["st_hs"] for ci in ids])
        g_ds = np.stack([ra[ci]["st_ds"] for ci in ids])
        g_hr = np.stack([ra[ci]["st_hr"] for ci in ids])
        mapsB = []
        for ci in ids:
            s = ci * T
            m = dict(com)
            m.update(lb)
            m.update({n: ra[ci][n] for n, _, _ in scratch_specs(NT)})
            m.update({"x": x[s:s + T], "g_hs": g_hs, "g_ds": g_ds, "g_hr": g_hr, "pm": pms[ci]})
            mapsB.append(m)
        rb = run_bass_kernel_spmd(ncB, mapsB, core_ids=ids).results
        x = np.concatenate([rb[ci]["xo"] for ci in ids], axis=0)
    return x[None].astype(np.float32)


def kernel(**inputs):
    return run_unfused(inputs, 16, NCORES, 4)
```
